# Optimizing a Trainium2 kernel written in Bass

```python
import jax, jax.numpy as jnp
from jax import lax
import numpy as np

D_MODEL = 1024
BATCH = 32
SEQ = 2048
DEPTH = 2
DEC_BATCH = 16
DEC_SEQ = 64
PAST_LEN = 2048

CHUNK = 64
N_MIXERS = 2
N_POOL_LAYERS = (DEPTH + 1) // 2
N_ATTN_LAYERS = DEPTH // 2
POOL_WINDOWS = (2, 4, 8, 16)
N_POOL_GROUPS = 4
POOL_GROUP = D_MODEL // N_POOL_GROUPS
POOL_STATE = max(POOL_WINDOWS) - 1
N_HEADS = 16
HEAD_DIM = D_MODEL // N_HEADS
N_KV_HEADS = 4
KV_GROUP = N_HEADS // N_KV_HEADS
N_IDX_HEADS = 8
IDX_DIM = 64
TOPK_MAX = 256
Q_BLOCK = 128
ROPE_THETA = 10000.0
D_FF = 4 * D_MODEL
EPS = 1e-6
NEG = -1e30
Q_W = N_HEADS * HEAD_DIM
KV_W = N_KV_HEADS * HEAD_DIM
QI_W = N_IDX_HEADS * IDX_DIM
PROJ_W = Q_W + 2 * KV_W + QI_W + IDX_DIM + N_IDX_HEADS

kernel_name = "pool_dsa_hybrid_stream_step"


def _rmsnorm(x, g):
    xf = x.astype(jnp.float32)
    xf = xf * lax.rsqrt(jnp.mean(xf * xf, axis=-1, keepdims=True) + EPS)
    return (xf * g.astype(jnp.float32)).astype(x.dtype)


def _rope(x, pos):
    d = x.shape[-1]
    inv_freq = 1.0 / (ROPE_THETA ** (jnp.arange(0, d, 2, dtype=jnp.float32) / d))
    ang = pos.astype(jnp.float32)[:, None] * inv_freq[None, :]
    c = jnp.cos(ang)[None, :, None, :]
    s = jnp.sin(ang)[None, :, None, :]
    xf = x.astype(jnp.float32)
    x1, x2 = xf[..., : d // 2], xf[..., d // 2:]
    return jnp.concatenate([x1 * c - x2 * s, x2 * c + x1 * s], axis=-1).astype(x.dtype)


def _pool_mixer(h, pos, past, w, scale):
    B, T, D = h.shape
    if past is None:
        past = jnp.zeros((B, POOL_STATE, D), h.dtype)
    padded = jnp.concatenate([past.astype(h.dtype), h], axis=1)
    cs = jnp.cumsum(padded.astype(jnp.float32), axis=1)
    cs0 = jnp.concatenate([jnp.zeros((B, 1, D), jnp.float32), cs], axis=1)
    P = POOL_STATE
    hf = h.astype(jnp.float32)
    outs = []
    for g, win in enumerate(POOL_WINDOWS):
        sl = slice(g * POOL_GROUP, (g + 1) * POOL_GROUP)
        wsum = cs0[:, P + 1: P + 1 + T, sl] - cs0[:, P + 1 - win: P + 1 - win + T, sl]
        cnt = jnp.minimum(pos + 1, win).astype(jnp.float32)[None, :, None]
        diff = (wsum / cnt - hf[..., sl]).astype(h.dtype)
        outs.append(diff @ w[g])
    y = jnp.concatenate(outs, axis=-1) * scale
    return y, padded[:, -POOL_STATE:]


def _dsa_attend(q, qi, wi, qpos, k, v, kidx, kpos, n_sel):
    B, T = q.shape[0], q.shape[1]
    logits = jnp.einsum('bthd,bsd->bths', qi.astype(jnp.float32), kidx.astype(jnp.float32)) * (IDX_DIM ** -0.5)
    score = jnp.einsum('bths,bth->bts', jax.nn.relu(logits), wi.astype(jnp.float32))
    qchunk = qpos // CHUNK
    adm = (kpos[None, :] // CHUNK) <= qchunk[:, None]
    score = jnp.where(adm[None], score, NEG)
    _, idx = lax.top_k(score, n_sel)
    valid = (kpos[idx] // CHUNK) <= qchunk[None, :, None]
    gather = jax.vmap(lambda a, i: a[i])
    kg = gather(k, idx)
    vg = gather(v, idx)
    qg = q.reshape(B, T, N_KV_HEADS, KV_GROUP, HEAD_DIM)
    s = jnp.einsum('btgrd,btkgd->btgrk', qg, kg).astype(jnp.float32) * (HEAD_DIM ** -0.5)
    s = jnp.where(valid[:, :, None, None, :], s, NEG)
    p = jax.nn.softmax(s, axis=-1).astype(v.dtype)
    o = jnp.einsum('btgrk,btkgd->btgrd', p, vg)
    return o.reshape(B, T, N_HEADS * HEAD_DIM)


def _attn_mixer(h, pos, past, w_in, w_o):
    B, T, _ = h.shape
    p = h @ w_in
    o0 = 0
    q = p[..., o0:o0 + Q_W].reshape(B, T, N_HEADS, HEAD_DIM); o0 += Q_W
    k = p[..., o0:o0 + KV_W].reshape(B, T, N_KV_HEADS, HEAD_DIM); o0 += KV_W
    v = p[..., o0:o0 + KV_W].reshape(B, T, N_KV_HEADS, HEAD_DIM); o0 += KV_W
    qi = p[..., o0:o0 + QI_W].reshape(B, T, N_IDX_HEADS, IDX_DIM); o0 += QI_W
    ki = p[..., o0:o0 + IDX_DIM].reshape(B, T, 1, IDX_DIM); o0 += IDX_DIM
    wi = p[..., o0:o0 + N_IDX_HEADS] * (N_IDX_HEADS ** -0.5)
    q = _rope(q, pos)
    k = _rope(k, pos)
    qi = _rope(qi, pos)
    ki = _rope(ki, pos)[:, :, 0]
    if past is None:
        k_all, v_all, ki_all = k, v, ki
    else:
        kc, vc, kic = past
        k_all = jnp.concatenate([kc, k.astype(kc.dtype)], axis=1)
        v_all = jnp.concatenate([vc, v.astype(vc.dtype)], axis=1)
        ki_all = jnp.concatenate([kic, ki.astype(kic.dtype)], axis=1)
    L = k_all.shape[1]
    kpos = jnp.arange(L, dtype=jnp.int32)
    n_sel = min(TOPK_MAX, L // 4)
    if T % Q_BLOCK == 0:
        nb = T // Q_BLOCK
        to_blocks = lambda a: jnp.moveaxis(a.reshape(B, nb, Q_BLOCK, *a.shape[2:]), 1, 0)
        xs = (to_blocks(q), to_blocks(qi), to_blocks(wi), pos.reshape(nb, Q_BLOCK))
        ob = lax.map(lambda a: _dsa_attend(a[0], a[1], a[2], a[3], k_all, v_all, ki_all, kpos, n_sel), xs)
        o = jnp.moveaxis(ob, 0, 1).reshape(B, T, N_HEADS * HEAD_DIM)
    else:
        o = _dsa_attend(q, qi, wi, pos, k_all, v_all, ki_all, kpos, n_sel)
    return o @ w_o, (k, v, ki)


def _mlp(h, w_up, w_down):
    u = jax.nn.relu(h @ w_up)
    return (u * u) @ w_down


def _trunk(x, pos, pool_past, attn_past, norm_mix, norm_mlp, norm_final, pool_w, pool_scale,
           attn_w_in, attn_w_o, mlp_w_up, mlp_w_down):
    pool_new, k_new, v_new, ki_new = [], [], [], []
    for i in range(DEPTH):
        h = _rmsnorm(x, norm_mix[i])
        j = i // N_MIXERS
        if i % N_MIXERS == 0:
            past = None if pool_past is None else pool_past[j]
            y, st = _pool_mixer(h, pos, past, pool_w[j], pool_scale[j])
            pool_new.append(st)
        else:
            past = None if attn_past is None else (attn_past[0][j], attn_past[1][j], attn_past[2][j])
            y, (kn, vn, kin) = _attn_mixer(h, pos, past, attn_w_in[j], attn_w_o[j])
            k_new.append(kn); v_new.append(vn); ki_new.append(kin)
        x = x + y.astype(x.dtype)
        x = x + _mlp(_rmsnorm(x, norm_mlp[i]), mlp_w_up[i], mlp_w_down[i]).astype(x.dtype)
    return (_rmsnorm(x, norm_final), jnp.stack(pool_new), jnp.stack(k_new), jnp.stack(v_new), jnp.stack(ki_new))


def setup_inputs(seed: int = 0) -> dict:
    key = jax.random.key(seed)
    ks = jax.random.split(key, 16)
    f32 = jnp.float32
    nrm = lambda k, shape, s: jax.random.normal(k, shape, f32) * s
    return {
        "x_prompt": nrm(ks[0], (BATCH, SEQ, D_MODEL), 1.0),
        "x_sample": nrm(ks[1], (DEC_BATCH, DEC_SEQ, D_MODEL), 1.0),
        "state_pool": nrm(ks[2], (N_POOL_LAYERS, DEC_BATCH, POOL_STATE, D_MODEL), 1.0),
        "cache_k": nrm(ks[3], (N_ATTN_LAYERS, DEC_BATCH, PAST_LEN, N_KV_HEADS, HEAD_DIM), 1.0),
        "cache_v": nrm(ks[4], (N_ATTN_LAYERS, DEC_BATCH, PAST_LEN, N_KV_HEADS, HEAD_DIM), 1.0),
        "cache_kidx": nrm(ks[5], (N_ATTN_LAYERS, DEC_BATCH, PAST_LEN, IDX_DIM), 1.0),
        "norm_mix": 1.0 + nrm(ks[6], (DEPTH, D_MODEL), 0.02),
        "norm_mlp": 1.0 + nrm(ks[7], (DEPTH, D_MODEL), 0.02),
        "norm_final": 1.0 + nrm(ks[8], (D_MODEL,), 0.02),
        "pool_w": nrm(ks[9], (N_POOL_LAYERS, N_POOL_GROUPS, POOL_GROUP, POOL_GROUP), POOL_GROUP ** -0.5),
        "pool_scale": 1.0 + nrm(ks[10], (N_POOL_LAYERS, D_MODEL), 0.02),
        "attn_w_in": nrm(ks[11], (N_ATTN_LAYERS, D_MODEL, PROJ_W), D_MODEL ** -0.5),
        "attn_w_o": nrm(ks[12], (N_ATTN_LAYERS, Q_W, D_MODEL), Q_W ** -0.5),
        "mlp_w_up": nrm(ks[13], (DEPTH, D_MODEL, D_FF), D_MODEL ** -0.5),
        "mlp_w_down": nrm(ks[14], (DEPTH, D_FF, D_MODEL), D_FF ** -0.5),
    }


def reference(x_prompt, x_sample, state_pool, cache_k, cache_v, cache_kidx, norm_mix, norm_mlp, norm_final,
              pool_w, pool_scale, attn_w_in, attn_w_o, mlp_w_up, mlp_w_down):
    T_p = x_prompt.shape[1]
    T_s = x_sample.shape[1]
    past_len = cache_k.shape[2]
    pos_p = jnp.arange(T_p, dtype=jnp.int32)
    pos_s = past_len + jnp.arange(T_s, dtype=jnp.int32)
    y_prompt, pool_p, k_p, v_p, ki_p = _trunk(
        x_prompt, pos_p, None, None, norm_mix, norm_mlp, norm_final, pool_w, pool_scale,
        attn_w_in, attn_w_o, mlp_w_up, mlp_w_down)
    y_sample, pool_s, k_s, v_s, ki_s = _trunk(
        x_sample, pos_s, state_pool, (cache_k, cache_v, cache_kidx), norm_mix, norm_mlp, norm_final,
        pool_w, pool_scale, attn_w_in, attn_w_o, mlp_w_up, mlp_w_down)
    return (y_prompt, y_sample, pool_p, pool_s, k_p, v_p, ki_p, k_s, v_s, ki_s)
```

```python
import numpy as np
import ml_dtypes
from contextlib import ExitStack
import concourse.bass as bass
import concourse.mybir as mybir
from concourse.bass_utils import run_bass_kernel_spmd

F32 = mybir.dt.float32
BF16 = mybir.dt.bfloat16
ALU = mybir.AluOpType
AF = mybir.ActivationFunctionType
AX = mybir.AxisListType

D = 1024
DFF = 4096
NH = 16
HD = 64
NKV = 4
NIH = 8
IDD = 64
CHUNK = 64
PSTATE = 15
NEGBIG = -1.0e30
MASKV = -30000.0
NBIS = 13
EPS = 1e-6


def mk(name, *a, **k):
    return lambda e: getattr(e, name)(*a, **k)


class Res:
    __slots__ = ("name", "w", "r")

    def __init__(self, name):
        self.name = name
        self.w = None
        self.r = []


class DSem:
    def __init__(self, sem, name):
        self.sem = sem
        self.cnt = 0
        self.name = name


class Prog:
    ENG = ("pe", "act", "dve", "pool", "sp")

    def __init__(self, nc, es):
        self.nc = nc
        self.es = es
        self.q = {e: [] for e in self.ENG}
        self.sem = {e: es.enter_context(nc.semaphore("sem_" + e)) for e in ("pe", "act", "dve", "pool")}
        self.cnt = {e: 0 for e in ("pe", "act", "dve", "pool")}
        self.seen = {e: {} for e in self.ENG}
        self.pend_r = []
        self.pend_w = []
        self.dsems = []
        self.nins = {e: 0 for e in self.ENG}
        self.rec = None

    def record(self, f, *a):
        self.rec = []
        f(*a)
        recs, self.rec = self.rec, None
        units, unit, pend = [], [], False
        self.marks = []
        for r in recs:
            if r[0] == "mark":
                assert not pend
                self.marks.append(len(units))
                continue
            unit.append(r)
            if r[0] == "op" and r[1] == "pe":
                pend = not r[5]
            if not pend:
                units.append(unit)
                unit = []
        assert not unit
        return units

    def mark(self):
        if self.rec is not None:
            self.rec.append(("mark",))

    def replay(self, unit):
        for r in unit:
            if r[0] == "op":
                self.op(r[1], r[2], r[3], r[4], r[5])
            else:
                self.dma(r[1], r[2], r[3], r[4], r[5], r[6])

    def sb(self, name, shape, dt):
        return self.es.enter_context(self.nc.sbuf_tensor(name, list(shape), dt))

    def ps(self, name, shape, dt):
        return self.es.enter_context(self.nc.psum_tensor(name, list(shape), dt))

    def dsem(self, name):
        d = DSem(self.es.enter_context(self.nc.semaphore("ds_" + name)), name)
        self.dsems.append(d)
        return d

    def _wait(self, eng, src, count):
        key = src if isinstance(src, str) else id(src)
        if self.seen[eng].get(key, 0) >= count:
            return
        self.seen[eng][key] = count
        sem = self.sem[src] if isinstance(src, str) else src.sem
        self.q[eng].append(mk("wait_ge", sem, count))

    def _deps(self, eng, reads, writes):
        if eng != "pe" and (self.pend_r or self.pend_w):
            pw = set(id(x) for x in self.pend_w)
            pa = pw | set(id(x) for x in self.pend_r)
            for R in reads:
                if id(R) in pw:
                    raise RuntimeError("read of PE-pending-written resource " + R.name)
            for R in writes:
                if id(R) in pa:
                    raise RuntimeError("write of PE-pending resource " + R.name)
        need = {}

        def add(s, c):
            if s == "pe" and eng == "pe":
                return
            k = s if isinstance(s, str) else id(s)
            if k not in need or need[k][1] < c:
                need[k] = (s, c)

        for R in reads:
            if R.w is not None:
                add(*R.w)
        for R in writes:
            if R.w is not None:
                add(*R.w)
            for (s, c) in R.r:
                add(s, c)
        for (s, c) in need.values():
            self._wait(eng, s, c)

    def op(self, eng, fn, reads=(), writes=(), inc=True):
        reads = list(reads)
        writes = list(writes)
        if self.rec is not None:
            self.rec.append(("op", eng, fn, reads, writes, inc))
            return
        self._deps(eng, reads, writes)
        self.nins[eng] += 1
        if eng == "pe" and not inc:
            self.q[eng].append(lambda e, fn=fn: fn(e))
            self.pend_r += reads
            self.pend_w += writes
            return
        self.cnt[eng] += 1
        c = self.cnt[eng]
        sem = self.sem[eng]
        self.q[eng].append(lambda e, fn=fn, sem=sem: fn(e).then_inc(sem, 1))
        if eng == "pe":
            reads = reads + self.pend_r
            writes = writes + self.pend_w
            self.pend_r = []
            self.pend_w = []
        for R in reads:
            R.r.append((eng, c))
            if len(R.r) > 16:
                R.r = self._compact(R.r)
        for R in writes:
            R.w = (eng, c)
            R.r = []

    @staticmethod
    def _compact(lst):
        best = {}
        for s, c in lst:
            k = s if isinstance(s, str) else id(s)
            if k not in best or best[k][1] < c:
                best[k] = (s, c)
        return list(best.values())

    def dma(self, eng, out, in_, ds, reads=(), writes=()):
        reads = list(reads)
        writes = list(writes)
        if self.rec is not None:
            self.rec.append(("dma", eng, out, in_, ds, reads, writes))
            return
        self._deps(eng, reads, writes)
        ds.cnt += 16
        c = ds.cnt
        self.nins[eng] += 1
        self.q[eng].append(lambda e, out=out, in_=in_, sem=ds.sem: e.dma_start(out=out, in_=in_).then_inc(sem, 16))
        for R in reads:
            R.r.append((ds, c))
        for R in writes:
            R.w = (ds, c)
            R.r = []

    def finish(self):
        assert not self.pend_r and not self.pend_w, "PE pending at finish"
        for d in self.dsems:
            if d.cnt > 0:
                self._wait("sp", d, d.cnt)
        nc = self.nc
        q = self.q
        with nc.Block() as blk:
            @blk.sync
            def _(e):
                for f in q["sp"]:
                    f(e)

            @blk.scalar
            def _(e):
                for f in q["act"]:
                    f(e)

            @blk.vector
            def _(e):
                for f in q["dve"]:
                    f(e)

            @blk.gpsimd
            def _(e):
                for f in q["pool"]:
                    f(e)

            @blk.tensor
            def _(e):
                for f in q["pe"]:
                    f(e)


def _q_perm():
    cols = []
    for j in range(8):
        base = 0 if j < 4 else 8
        jj = j % 4
        for h in (base + jj, base + 4 + jj):
            cols += list(range(h * HD, (h + 1) * HD))
    return np.array(cols)


def _bands():
    B = np.zeros((5, 4, 128, 128), np.float32)
    for g, win in enumerate((2, 4, 8, 16)):
        for t in range(128):
            for j in range(win):
                tp = t - j
                if tp >= 0:
                    B[0, g, tp, t] += 1.0 / win
                else:
                    B[1, g, 128 + tp, t] += 1.0 / win
            B[0, g, t, t] -= 1.0
            cnt = min(t + 1, win)
            for j in range(win):
                tp = t - j
                if tp >= 0:
                    B[2, g, tp, t] += 1.0 / cnt
            B[2, g, t, t] -= 1.0
        for t in range(64):
            for j in range(win):
                tp = t - j
                if tp >= 0:
                    B[3, g, tp, t] += 1.0 / win
                else:
                    B[4, g, PSTATE + tp, t] += 1.0 / win
            B[3, g, t, t] -= 1.0
    return B


def _rope_tab(pos):
    inv = 1.0 / (10000.0 ** (np.arange(0, HD, 2, dtype=np.float32) / HD))
    ang = pos.astype(np.float32)[:, None] * inv[None, :]
    return np.cos(ang).astype(np.float32), np.sin(ang).astype(np.float32)


class StopBuild(Exception):
    pass


def build(cfg):
    try:
        return _build(cfg)
    except StopBuild as sb_:
        nc, es, P = sb_.args
        P.finish()
        return nc, es, P


def _build(cfg):
    NPS, T, NSS, TS, PL = cfg["NPS"], cfg["T"], cfg["NSS"], cfg["TS"], cfg["PL"]
    assert TS == 64 and T % 128 == 0 and PL % 128 == 0
    PT = min(1024, T)
    NTP = PT // 128
    LS = PL + TS
    LKMAX = max(T, LS)
    NKT = (LKMAX + 127) // 128
    KSEL_P = min(256, T // 4)
    KSEL_S = min(256, LS // 4)

    nc = bass.Bass("TRN2", target_bir_lowering=False)

    def din(name, shape, dt=F32):
        return nc.dram_tensor(name, list(shape), dt, kind="ExternalInput").ap()

    def dout(name, shape, dt=F32):
        return nc.dram_tensor(name, list(shape), dt, kind="ExternalOutput").ap()

    xp = din("xp", [NPS, T, D])
    xs = din("xs", [NSS, TS, D])
    spool = din("spool", [NSS, PSTATE, D])
    ck = din("ck", [NSS, PL, 256])
    cv = din("cv", [NSS, PL, 256])
    cki = din("cki", [NSS, PL, 64])
    g_all = din("g_all", [5, D])
    poolw = din("poolw", [4, 256, 256])
    pscale = din("pscale", [D])
    w_kv = din("w_kv", [D, 576])
    w_q = din("w_q", [D, 1544])
    w_o = din("w_o", [D, D])
    w_up = din("w_up", [2, D, DFF])
    w_dn = din("w_dn", [2, DFF, D])
    c_ident = din("c_ident", [128, 128])
    c_i4 = din("c_i4", [128, 512])
    c_bands = din("c_bands", [128, 20, 128])
    c_cosp = din("c_cosp", [T, 32])
    c_sinp = din("c_sinp", [T, 32])
    c_coss = din("c_coss", [TS, 32])
    c_sins = din("c_sins", [TS, 32])
    c_pow2 = din("c_pow2", [128, 2 * (NBIS + 1)])

    y_p = dout("y_p", [NPS, T, D])
    y_s = dout("y_s", [NSS, TS, D])
    pool_p = dout("pool_p", [NPS, PSTATE, D])
    pool_s = dout("pool_s", [NSS, PSTATE, D])
    k_p = dout("k_p", [NPS, T, 256])
    v_p = dout("v_p", [NPS, T, 256])
    ki_p = dout("ki_p", [NPS, T, 64])
    k_s = dout("k_s", [NSS, TS, 256])
    v_s = dout("v_s", [NSS, TS, 256])
    ki_s = dout("ki_s", [NSS, TS, 64])

    es = ExitStack()
    P = Prog(nc, es)
    npasses = NPS * (T // PT) + (1 if NSS > 0 else 0)

    X = P.sb("X", [128, NTP, D], F32)
    hT = P.sb("hT", [128, 8, PT], BF16)
    hn = [P.sb("hn%d" % i, [128, D], BF16) for i in range(2)]
    stg = [P.sb("stg%d" % i, [128, D], F32) for i in range(2)]
    gbc = [P.sb("gbc%d" % i, [128, D], F32) for i in range(2)]
    WR = P.sb("WR", [128, 16960], BF16)
    wst = [P.sb("wst%d" % i, [128, 1024], F32) for i in range(2)]
    ident = P.sb("ident", [128, 128], BF16)
    i4 = P.sb("i4", [128, 512], BF16)
    bands = P.sb("bands", [128, 20, 128], BF16)
    poolw_sb = P.sb("poolw_sb", [128, 8, 256], BF16)
    cosT = P.sb("cosT", [128, NTP, 32], F32)
    sinT = P.sb("sinT", [128, NTP, 32], F32)
    pow2 = P.sb("pow2", [128, 2 * (NBIS + 1)], F32)
    stat = P.sb("stat", [128, 64], F32)
    epsT = P.sb("epsT", [128, 1], F32)
    pastb = P.sb("pastb", [16, D], BF16)
    hprev = P.sb("hprev", [128, D], BF16)
    rhprev = Res("hprev")
    uT = [P.sb("uT%d" % i, [128, 4, 512], BF16) for i in range(2)]
    kT2 = P.sb("kT2", [128, 2, NKT * 128], BF16)
    kiT2 = P.sb("kiT2", [128, NKT * 128], BF16)
    Vp = P.sb("Vp", [128, NKT, 4 * 65], BF16)
    score_ = [P.sb("score%d" % i, [128, NKT * 128], F32) for i in range(2)]
    mb_ = [P.sb("mb%d" % i, [128, NKT * 128], BF16) for i in range(2)]
    rtmp = [P.sb("rtmp%d" % i, [128, 512], F32) for i in range(2)]
    pTb = [P.sb("pTb%d" % i, [128, 512], BF16) for i in range(3)]
    qro = P.sb("qro", [128, 1536], BF16)
    kvo = P.sb("kvo", [128, 576], F32)
    kdup = P.sb("kdup", [128, 384], BF16)
    qT_ = [P.sb("qT%d" % i, [128, 8, 128], BF16) for i in range(3)]
    qiT = P.sb("qiT", [128, 4, 128], BF16)
    ob = P.sb("ob", [128, D], BF16)
    wi_ = [P.sb("wi_sb%d" % i, [128, 8], F32) for i in range(2)]
    bisA_ = [P.sb("bisA%d" % i, [128, 2 * (NBIS + 1)], F32) for i in range(2)]
    bis_ = [P.sb("bis%d" % i, [128, 8], F32) for i in range(2)]
    rcp = P.sb("rcp", [128, 4], F32)

    TB = [P.ps("TB%d" % i, [128, 1024], BF16) for i in range(2)]
    FB = [P.ps("FB%d" % i, [128, 512], F32) for i in range(6)]
    rTB = [Res("TB%d" % i) for i in range(2)]
    rFB = [Res("FB%d" % i) for i in range(6)]

    rX = [Res("X%d" % i) for i in range(NTP)]
    rhT = [Res("hT%d" % i) for i in range(NTP)]
    rhn = [Res("hn%d" % i) for i in range(2)]
    rstg = [Res("stg%d" % i) for i in range(2)]
    rgbc = [Res("gbc%d" % i) for i in range(2)]
    rWS = [Res("WS%d" % i) for i in range(4)]
    rwst = [Res("wst%d" % i) for i in range(2)]
    rconst = Res("const")
    rtab = Res("ropetab")
    rstat = Res("stat")
    rstatc = [Res("statc%d" % i) for i in range(64)]
    rpast = Res("pastb")
    ruT = [Res("uT%d" % i) for i in range(2)]
    rK = [Res("K%d" % i) for i in range(NKT)]
    rscore_ = [Res("score%d" % i) for i in range(2)]
    rmb_ = [Res("mb%d" % i) for i in range(2)]
    rrtmp = [Res("rtmp%d" % i) for i in range(2)]
    rpTb = [Res("pTb%d" % i) for i in range(3)]
    rqro = Res("qro")
    rkvo = Res("kvo")
    rkdup = Res("kdup")
    rqT_ = [Res("qT%d" % i) for i in range(3)]
    rqiT = Res("qiT")
    rob = Res("ob")
    rwi_ = [Res("wi%d" % i) for i in range(2)]
    rbis_ = [Res("bis%d" % i) for i in range(2)]
    rrcp = Res("rcp")

    dsX = [P.dsem("X%d" % i) for i in range(NTP)]
    dsW = [P.dsem("wst%d" % i) for i in range(2)]
    dsG = [P.dsem("gbc%d" % i) for i in range(2)]
    dsC = P.dsem("const")
    dsTab = P.dsem("tab")
    dsStg = [P.dsem("stg%d" % i) for i in range(2)]
    dsKvo = P.dsem("kvo")
    dsPast = P.dsem("past")
    for R, d in zip(rX, dsX):
        pass

    P.dma("sp", wst[0][:, 0:128], c_ident[:, :], dsW[0], writes=[rwst[0]])
    P.op("dve", mk("tensor_copy", out=ident[:], in_=wst[0][:, 0:128]), reads=[rwst[0]], writes=[rconst])
    P.dma("sp", wst[1][:, 0:512], c_i4[:, :], dsW[1], writes=[rwst[1]])
    P.op("dve", mk("tensor_copy", out=i4[:], in_=wst[1][:, 0:512]), reads=[rwst[1]], writes=[rconst])
    for bq in range(5):
        P.dma("sp", wst[0][:, 0:512].rearrange("p (a b) -> p a b", a=4), c_bands[:, bq * 4:(bq + 1) * 4, :], dsW[0], writes=[rwst[0]])
        P.op("dve", mk("tensor_copy", out=bands[:, bq * 4:(bq + 1) * 4, :], in_=wst[0][:, 0:512].rearrange("p (a b) -> p a b", a=4)),
             reads=[rwst[0]], writes=[rconst])
    P.dma("sp", pow2[:], c_pow2[:, :], dsC, writes=[rconst])
    psc_bc = gbc[1]
    P.dma("sp", psc_bc[:], pscale.partition_broadcast(128), dsG[1], writes=[rgbc[1]])
    P.op("dve", mk("memset", epsT[:], EPS), writes=[rconst])
    P.op("dve", mk("memset", stat[:], 1.0), writes=rstatc)
    P.op("dve", mk("memset", Vp[:], 1.0), writes=rK)
    P.op("dve", mk("memset", kT2[:], 0.0), writes=rK)
    P.op("dve", mk("memset", kiT2[:], 0.0), writes=rK)
    for bi0 in range(2):
        P.op("dve", mk("memset", mb_[bi0][:], 0.0), writes=[rmb_[bi0]])
        P.op("dve", mk("memset", qT_[bi0][:], 0.0), writes=[rqT_[bi0]])
    P.op("dve", mk("memset", qT_[2][:], 0.0), writes=[rqT_[2]])
    wsi = [0]

    def next_wst():
        i = wsi[0] % 2
        wsi[0] += 1
        return i

    for g in range(4):
        for c in range(2):
            i = next_wst()
            P.dma("sp", wst[i][:, 0:256], poolw[g, c * 128:(c + 1) * 128, :], dsW[i], writes=[rwst[i]])
            P.op("dve", mk("tensor_tensor",
                out=poolw_sb[:, g * 2 + c, :], in0=wst[i][:, 0:256], in1=psc_bc[:, g * 256:(g + 1) * 256], op=ALU.mult),
                reads=[rwst[i], rgbc[1]], writes=[rconst])

    cast_rr = [0]

    scr = {}
    rScrAll = Res("scratch_all")
    dsScr = P.dsem("scr")
    dsWS = [P.dsem("ws%d" % i) for i in range(5)]
    cur_pass = [0]

    def load_w(dst_fn, src_fn, nchunks, width, wres, name=None, region=None, dsi=0):
        if name is not None and cur_pass[0] > 0:
            P.dma("pool", region, scr[name], dsWS[dsi], reads=[rScrAll], writes=wres)
            return
        for j in range(nchunks):
            i = next_wst()
            P.dma("sp", wst[i][:, 0:width], src_fn(j), dsW[i], writes=[rwst[i]])
            P.op("pool", mk("tensor_copy", out=dst_fn(j), in_=wst[i][:, 0:width]),
                 reads=[rwst[i]], writes=wres)
        if name is not None and npasses > 1:
            ncols = region.shape[1]
            scr[name] = nc.dram_tensor("scr_" + name, [128, ncols], BF16, kind="Internal").ap()
            P.dma("sp", scr[name], region, dsScr, reads=wres, writes=[rScrAll])

    def WRv(off, k, n):
        return WR[:, off:off + k * n].rearrange("p (k n) -> p k n", k=k)

    def load_g(idx, slot):
        P.dma("sp", gbc[slot][:], g_all[idx].partition_broadcast(128), dsG[slot], writes=[rgbc[slot]])

    junk = P.sb("junk", [128, D], BF16)
    rjunk = Res("junk")

    def norm_stats(tiles, col0):
        nt_ = len(tiles)
        for i, tl in enumerate(tiles):
            n = tl["n"]
            P.op("act", mk("activation",
                out=junk[0:n, :], in_=X[0:n, i, :], func=AF.Square, accum_out=stat[0:n, col0 + i:col0 + i + 1]),
                reads=[rX[i]], writes=[rjunk, rstatc[col0 + i]])
        cols = rstatc[col0:col0 + nt_]
        P.op("act", mk("activation", out=stat[:, col0:col0 + nt_], in_=stat[:, col0:col0 + nt_], func=AF.Sqrt,
                       scale=1.0 / D, bias=epsT[:]), reads=cols + [rconst], writes=cols)
        P.op("dve", mk("reciprocal", out=stat[:, col0:col0 + nt_], in_=stat[:, col0:col0 + nt_]),
             reads=cols, writes=cols)

    def norm_tile(i, n, col):
        pass

    def normalize(i, n, col, gslot, out_ap, out_res):
        P.op("dve", mk("scalar_tensor_tensor",
            out=out_ap, in0=X[0:n, i, :], scalar=stat[0:n, col:col + 1], in1=gbc[gslot][0:n, :],
            op0=ALU.mult, op1=ALU.mult), reads=[rX[i], rstatc[col], rgbc[gslot]], writes=[out_res])

    tbi = [0]

    def transpose_to_hT(i, n, src, src_res, c0):
        for half in range(2):
            b = tbi[0] % 2
            tbi[0] += 1
            for c in range(4):
                cc = half * 4 + c
                P.op("pe", mk("transpose",
                    out=TB[b][:, c * 128:c * 128 + n], in_=src[0:n, cc * 128:(cc + 1) * 128], identity=ident[0:n, 0:n]),
                    reads=[src_res, rconst], writes=[rTB[b]], inc=(c == 3))
            P.op("act", mk("activation",
                out=hT[:, half * 4:(half + 1) * 4, c0:c0 + n],
                in_=TB[b][:, 0:512].rearrange("p (c t) -> p c t", c=4)[:, :, 0:n], func=AF.Copy),
                reads=[rTB[b]], writes=[rhT[i]])

    passes = []
    for b in range(NPS):
        for p0 in range(0, T, PT):
            tiles = []
            for i in range(PT // 128):
                t0 = p0 + i * 128
                tiles.append(dict(kind="p", n=128, seq=b, t0=t0, gi=t0 // 128,
                                  xsrc=xp[b, t0:t0 + 128, :], ydst=y_p[b, t0:t0 + 128, :]))
            passes.append(dict(kind="p", seq=b, p0=p0, tiles=tiles))
    if NSS > 0:
        tiles = []
        for s in range(NSS):
            tiles.append(dict(kind="s", n=64, seq=s, t0=0, gi=PL // 128,
                              xsrc=xs[s, :, :], ydst=y_s[s, :, :]))
        passes.append(dict(kind="s", seq=0, p0=0, tiles=tiles))

    hn_i = [0]
    stg_i = [0]
    fbr = [0]
    prev_hn = [None]

    def run_mlp(layer, tiles, gslot_col):
        ntok = sum(t["n"] for t in tiles)
        blocks = []
        c0 = 0
        cur = []
        cn = 0
        off = 0
        for i, tl in enumerate(tiles):
            cur.append((i, cn, tl["n"]))
            cn += tl["n"]
            if cn >= 512 or i == len(tiles) - 1:
                blocks.append((c0, cn, cur))
                c0 += cn
                cur = []
                cn = 0
        NG = DFF // 512
        for G in range(NG):
            sl = G % 2
            offu = sl * 8192
            offd = sl * 8192 + 4096
            wup_v = WRv(offu, 8, 512)
            wdn_v = WRv(offd, 4, 1024)
            load_w(lambda j: wup_v[:, j, :],
                   lambda j: w_up[layer, j * 128:(j + 1) * 128, G * 512:(G + 1) * 512], 8, 512, [rWS[sl * 2]],
                   name="up%d_%d" % (layer, G), region=WR[:, offu:offu + 4096], dsi=sl * 2)
            load_w(lambda j: wdn_v[:, j, :],
                   lambda j: w_dn[layer, G * 512 + j * 128:G * 512 + (j + 1) * 128, :], 4, 1024, [rWS[sl * 2 + 1]],
                   name="dn%d_%d" % (layer, G), region=WR[:, offd:offd + 4096], dsi=sl * 2 + 1)
            for (bc0, bn, btiles) in blocks:
                ub = fbr[0] % 2
                fbr[0] += 1
                for m in range(4):
                    fb = m % 3
                    for k in range(8):
                        P.op("pe", mk("matmul",
                            FB[fb][:, 0:bn], wup_v[:, k, m * 128:(m + 1) * 128], hT[:, k, bc0:bc0 + bn],
                            start=(k == 0), stop=(k == 7)),
                            reads=[rWS[sl * 2]] + [rhT[i] for (i, _, _) in btiles], writes=[rFB[fb]], inc=(k == 7))
                    rt = m % 2
                    P.op("act", mk("activation",
                        out=pTb[rt][:, 0:bn],
                        in_=FB[fb][:, 0:bn], func=AF.Relu), reads=[rFB[fb]], writes=[rpTb[rt]])
                    P.op("dve", mk("tensor_tensor",
                        out=uT[ub][:, m, 0:bn], in0=pTb[rt][:, 0:bn], in1=pTb[rt][:, 0:bn], op=ALU.mult),
                        reads=[rpTb[rt]], writes=[ruT[ub]])
                for (i, r0, n) in btiles:
                    for hf in range(2):
                        fb = 3 + (fbr[0] % 3)
                        fbr[0] += 1
                        for m in range(4):
                            P.op("pe", mk("matmul",
                                FB[fb][0:n, :], uT[ub][:, m, r0:r0 + n], wdn_v[:, m, hf * 512:(hf + 1) * 512],
                                start=(m == 0), stop=(m == 3)),
                                reads=[ruT[ub], rWS[sl * 2 + 1]], writes=[rFB[fb]], inc=(m == 3))
                        P.op("dve", mk("tensor_tensor",
                            out=X[0:n, i, hf * 512:(hf + 1) * 512], in0=FB[fb][0:n, :],
                            in1=X[0:n, i, hf * 512:(hf + 1) * 512], op=ALU.add),
                            reads=[rFB[fb], rX[i]], writes=[rX[i]])

    def norm_to_hT(tiles, gidx, gslot):
        load_g(gidx, gslot)
        norm_stats(tiles, 16 * (gidx % 2 + 1))
        c0 = 0
        for i, tl in enumerate(tiles):
            n = tl["n"]
            h = hn_i[0] % 2
            hn_i[0] += 1
            norm_tile(i, n, 16 * (gidx % 2 + 1) + i)
            normalize(i, n, 16 * (gidx % 2 + 1) + i, gslot, hn[h][0:n, :], rhn[h])
            transpose_to_hT(i, n, hn[h], rhn[h], c0)
            c0 += n

    if cfg.get("only"):
        passes = [p_ for p_ in passes if p_["kind"] == cfg["only"]]
    for pi_, ps_ in enumerate(passes):
        cur_pass[0] = pi_
        tiles = ps_["tiles"]
        nt = len(tiles)
        kind = ps_["kind"]
        for i, tl in enumerate(tiles):
            n = tl["n"]
            P.dma("sp", X[0:n, i, :], tl["xsrc"], dsX[i], writes=[rX[i]])
        if kind == "p":
            p0 = ps_["p0"]
            P.dma("sp", cosT[:, 0:nt, :], c_cosp[p0:p0 + PT, :].rearrange("(i p) f -> p i f", p=128), dsTab, writes=[rtab])
            P.dma("sp", sinT[:, 0:nt, :], c_sinp[p0:p0 + PT, :].rearrange("(i p) f -> p i f", p=128), dsTab, writes=[rtab])
        else:
            for i in range(nt):
                P.dma("sp", cosT[0:64, i, :], c_coss[:, :], dsTab, writes=[rtab])
                P.dma("sp", sinT[0:64, i, :], c_sins[:, :], dsTab, writes=[rtab])

        if cfg.get("stop") == "load":
            P.finish()
            return nc, es, P
        load_g(0, 0)
        norm_stats(tiles, 0)
        for i, tl in enumerate(tiles):
            n = tl["n"]
            h = i % 2
            normalize(i, n, i, 0, hn[h][0:n, :], rhn[h])
            if kind == "p" and i == nt - 1 and tl["t0"] + 128 < T:
                P.op("pool", mk("tensor_copy", out=hprev[:, :], in_=hn[h][:, :]), reads=[rhn[h]], writes=[rhprev])
            want_state = (kind == "s") or (tl["t0"] + 128 == T)
            if want_state:
                s_ = stg_i[0] % 2
                stg_i[0] += 1
                normalize(i, n, i, 0, stg[s_][0:n, :], rstg[s_])
                dst = pool_s[tl["seq"], :, :] if kind == "s" else pool_p[tl["seq"], :, :]
                P.dma("sp", dst, stg[s_][n - PSTATE:n, :], dsStg[s_], reads=[rstg[s_]])
            if kind == "s":
                i_w = next_wst()
                P.dma("sp", wst[i_w][0:PSTATE, :], spool[tl["seq"], :, :], dsW[i_w], writes=[rwst[i_w]])
                P.op("dve", mk("tensor_copy", out=pastb[0:PSTATE, :], in_=wst[i_w][0:PSTATE, :]),
                     reads=[rwst[i_w]], writes=[rpast])
            for c in range(8):
                g = c // 2
                fb = c // 4
                col = (c % 4) * 128
                if kind == "p":
                    first = (tl["t0"] == 0)
                    if first:
                        P.op("pe", mk("matmul",
                            FB[fb][:, col:col + 128], hn[h][:, c * 128:(c + 1) * 128], bands[:, 8 + g, :],
                            start=True, stop=True), reads=[rhn[h], rconst], writes=[rFB[fb]], inc=(c % 4 == 3))
                    else:
                        if i > 0:
                            pv_ap, pv_res = hn[1 - h], rhn[1 - h]
                        else:
                            pv_ap, pv_res = hprev, rhprev
                        P.op("pe", mk("matmul",
                            FB[fb][:, col:col + 128], pv_ap[:, c * 128:(c + 1) * 128], bands[:, 4 + g, :],
                            start=True, stop=False), reads=[pv_res, rconst], writes=[rFB[fb]], inc=False)
                        P.op("pe", mk("matmul",
                            FB[fb][:, col:col + 128], hn[h][:, c * 128:(c + 1) * 128], bands[:, g, :],
                            start=False, stop=True), reads=[rhn[h], rconst], writes=[rFB[fb]], inc=(c % 4 == 3))
                else:
                    P.op("pe", mk("matmul",
                        FB[fb][:, col:col + 64], pastb[0:PSTATE, c * 128:(c + 1) * 128], bands[0:PSTATE, 16 + g, 0:64],
                        start=True, stop=False), reads=[rpast, rconst], writes=[rFB[fb]], inc=False)
                    P.op("pe", mk("matmul",
                        FB[fb][:, col:col + 64], hn[h][0:64, c * 128:(c + 1) * 128], bands[0:64, 12 + g, 0:64],
                        start=False, stop=True), reads=[rhn[h], rconst], writes=[rFB[fb]], inc=(c % 4 == 3))
            for fb in range(2):
                P.op("act", mk("activation",
                    out=pTb[fb][:, :].rearrange("p (c t) -> p c t", c=4)[:, :, 0:n],
                    in_=FB[fb][:, :].rearrange("p (c t) -> p c t", c=4)[:, :, 0:n], func=AF.Copy),
                    reads=[rFB[fb]], writes=[rpTb[fb]])
            for g in range(4):
                fb = 2 + g // 2
                col = (g % 2) * 256
                for cc in range(2):
                    c = g * 2 + cc
                    P.op("pe", mk("matmul",
                        FB[fb][0:n, col:col + 256],
                        pTb[c // 4][:, :].rearrange("p (c t) -> p c t", c=4)[:, c % 4, 0:n], poolw_sb[:, c, :],
                        start=(cc == 0), stop=(cc == 1)),
                        reads=[rpTb[c // 4], rconst], writes=[rFB[fb]], inc=(cc == 1 and g % 2 == 1))
            for hf in range(2):
                P.op("dve", mk("tensor_tensor",
                    out=X[0:n, i, hf * 512:(hf + 1) * 512], in0=FB[2 + hf][0:n, :],
                    in1=X[0:n, i, hf * 512:(hf + 1) * 512], op=ALU.add),
                    reads=[rFB[2 + hf], rX[i]], writes=[rX[i]])

        if cfg.get("stop") == "l0":
            P.finish()
            return nc, es, P
        norm_to_hT(tiles, 1, 1)
        run_mlp(0, tiles, 0)

        if cfg.get("stop") == "mlp0":
            P.finish()
            return nc, es, P
        norm_to_hT(tiles, 2, 0)
        wkv_v = WRv(0, 8, 576)
        wq_v = WR[:, 4608:4608 + 8 * 1544].rearrange("p (k n) -> p k n", k=8)
        allWS = rWS
        load_w(lambda j: wkv_v[:, j, :], lambda j: w_kv[j * 128:(j + 1) * 128, :], 8, 576, allWS,
               name="wkv", region=WR[:, 0:4608], dsi=0)
        if cur_pass[0] > 0:
            load_w(None, None, 0, 0, allWS, name="wq", region=WR[:, 4608:4608 + 12352], dsi=1)
        else:
            for part in range(2):
                load_w(lambda j: wq_v[:, j, part * 772:(part + 1) * 772],
                       lambda j: w_q[j * 128:(j + 1) * 128, part * 772:(part + 1) * 772], 8, 772, allWS,
                       name=("wq" if part == 1 else None), region=WR[:, 4608:4608 + 12352], dsi=1)

        def rope(eng_unused, src_ps, ncols_heads, n, i, dst, dst_res, src_res, tmpa, tmpb, rta, rtb):
            H = ncols_heads
            sv = src_ps.rearrange("p (h two f) -> p h two f", h=H, two=2)
            dv = dst.rearrange("p (h two f) -> p h two f", h=H, two=2)
            cb = cosT[0:n, i, :].unsqueeze(1).to_broadcast([n, H, 32])
            sb_ = sinT[0:n, i, :].unsqueeze(1).to_broadcast([n, H, 32])
            ta = tmpa[0:n, 0:H * 32].rearrange("p (h f) -> p h f", h=H)
            tb = tmpb[0:n, 0:H * 32].rearrange("p (h f) -> p h f", h=H)
            x1 = sv[:, :, 0, :]
            x2 = sv[:, :, 1, :]
            P.op("dve", mk("tensor_tensor", out=ta, in0=x1, in1=cb, op=ALU.mult), reads=[src_res, rtab], writes=[rta])
            P.op("dve", mk("tensor_tensor", out=tb, in0=x2, in1=sb_, op=ALU.mult), reads=[src_res, rtab], writes=[rtb])
            P.op("dve", mk("tensor_tensor", out=dv[:, :, 0, :], in0=ta, in1=tb, op=ALU.subtract),
                 reads=[rta, rtb], writes=[dst_res])
            P.op("dve", mk("tensor_tensor", out=ta, in0=x2, in1=cb, op=ALU.mult), reads=[src_res, rtab], writes=[rta])
            P.op("dve", mk("tensor_tensor", out=tb, in0=x1, in1=sb_, op=ALU.mult), reads=[src_res, rtab], writes=[rtb])
            P.op("dve", mk("tensor_tensor", out=dv[:, :, 1, :], in0=ta, in1=tb, op=ALU.add),
                 reads=[rta, rtb], writes=[dst_res])

        def keys_from_tokmajor(kt, n):
            b = tbi[0] % 2
            tbi[0] += 1
            for c in range(3):
                P.op("pe", mk("transpose",
                    out=TB[b][:, c * 128:c * 128 + n], in_=kdup[0:n, c * 128:(c + 1) * 128], identity=ident[0:n, 0:n]),
                    reads=[rkdup, rconst], writes=[rTB[b]], inc=(c == 2))
            P.op("act", mk("activation",
                out=kT2[:, :, kt * 128:kt * 128 + n],
                in_=TB[b][:, 0:256].rearrange("p (c t) -> p c t", c=2)[:, :, 0:n], func=AF.Copy),
                reads=[rTB[b]], writes=[rK[kt]])
            P.op("act", mk("activation",
                out=kiT2[:, kt * 128:kt * 128 + n], in_=TB[b][:, 256:256 + n], func=AF.Copy),
                reads=[rTB[b]], writes=[rK[kt]])

        c0s = []
        acc_ = 0
        for tl_ in tiles:
            c0s.append(acc_)
            acc_ += tl_["n"]

        def f1(i):
            tl = tiles[i]
            c0 = c0s[i]
            bi = i % 2
            n = tl["n"]
            gi = tl["gi"]
            seq = tl["seq"]
            if kind == "s":
                for kt in range(PL // 128):
                    iw = next_wst()
                    P.dma("sp", wst[iw][:, 0:256], ck[seq, kt * 128:(kt + 1) * 128, :], dsW[iw], writes=[rwst[iw]])
                    P.dma("sp", wst[iw][:, 256:320], cki[seq, kt * 128:(kt + 1) * 128, :], dsW[iw], writes=[rwst[iw]])
                    P.dma("sp", wst[iw][:, 320:576], cv[seq, kt * 128:(kt + 1) * 128, :], dsW[iw], writes=[rwst[iw]])
                    P.op("dve", mk("tensor_copy", out=kdup[:, 0:256], in_=wst[iw][:, 0:256]),
                         reads=[rwst[iw]], writes=[rkdup])
                    P.op("dve", mk("tensor_copy",
                        out=kdup[:, 256:384].rearrange("p (two f) -> p two f", two=2),
                        in_=wst[iw][:, 256:320].unsqueeze(1).to_broadcast([128, 2, 64])),
                        reads=[rwst[iw]], writes=[rkdup])
                    P.op("dve", mk("tensor_copy",
                        out=Vp[:, kt, :].rearrange("p (g f) -> p g f", g=4)[:, :, 0:64],
                        in_=wst[iw][:, 320:576].rearrange("p (g f) -> p g f", g=4)),
                        reads=[rwst[iw]], writes=[rK[kt]])
                    keys_from_tokmajor(kt, 128)
            if cfg.get("stop") == "a0":
                raise StopBuild(nc, es, P)
            for k in range(8):
                P.op("pe", mk("matmul", FB[4][0:n, 0:320], hT[:, k, c0:c0 + n], wkv_v[:, k, 0:320],
                                                    start=(k == 0), stop=(k == 7)),
                     reads=[rhT[i]] + allWS, writes=[rFB[4]], inc=(k == 7))
            for k in range(8):
                P.op("pe", mk("matmul", FB[5][0:n, 0:256], hT[:, k, c0:c0 + n], wkv_v[:, k, 320:576],
                                                    start=(k == 0), stop=(k == 7)),
                     reads=[rhT[i]] + allWS, writes=[rFB[5]], inc=(k == 7))
            rope(None, FB[4][0:n, 0:320], 5, n, i, kvo[0:n, 0:320], rkvo, rFB[4], rtmp[0], rtmp[1], rrtmp[0], rrtmp[1])
            P.op("act", mk("activation", out=kvo[0:n, 320:576], in_=FB[5][0:n, 0:256], func=AF.Copy),
                 reads=[rFB[5]], writes=[rkvo])
            kt = gi
            if kind == "p":
                b_ = seq
                t0 = tl["t0"]
                P.dma("sp", k_p[b_, t0:t0 + 128, :], kvo[:, 0:256], dsKvo, reads=[rkvo])
                P.dma("sp", ki_p[b_, t0:t0 + 128, :], kvo[:, 256:320], dsKvo, reads=[rkvo])
                P.dma("sp", v_p[b_, t0:t0 + 128, :], kvo[:, 320:576], dsKvo, reads=[rkvo])
            else:
                P.dma("sp", k_s[seq, :, :], kvo[0:64, 0:256], dsKvo, reads=[rkvo])
                P.dma("sp", ki_s[seq, :, :], kvo[0:64, 256:320], dsKvo, reads=[rkvo])
                P.dma("sp", v_s[seq, :, :], kvo[0:64, 320:576], dsKvo, reads=[rkvo])
            P.op("dve", mk("tensor_copy", out=kdup[0:n, 0:256], in_=kvo[0:n, 0:256]), reads=[rkvo], writes=[rkdup])
            P.op("dve", mk("tensor_copy",
                out=kdup[0:n, 256:384].rearrange("p (two f) -> p two f", two=2),
                in_=kvo[0:n, 256:320].unsqueeze(1).to_broadcast([n, 2, 64])), reads=[rkvo], writes=[rkdup])
            P.op("dve", mk("tensor_copy",
                out=Vp[0:n, kt, :].rearrange("p (g f) -> p g f", g=4)[:, :, 0:64],
                in_=kvo[0:n, 320:576].rearrange("p (g f) -> p g f", g=4)), reads=[rkvo], writes=[rK[kt]])
            keys_from_tokmajor(kt, n)

            if cfg.get("stop") == "a":
                raise StopBuild(nc, es, P)
            ladm = gi * 128 + n if kind == "p" else LS
            ksel = KSEL_P if kind == "p" else KSEL_S
            nkb = (ladm + 127) // 128
            for hf in range(2):
                for k in range(8):
                    P.op("pe", mk("matmul",
                        FB[4 + hf][0:n, :], hT[:, k, c0:c0 + n], wq_v[:, k, hf * 512:(hf + 1) * 512],
                        start=(k == 0), stop=(k == 7)), reads=[rhT[i]] + allWS, writes=[rFB[4 + hf]], inc=(k == 7))
            P.mark()
            for hf in range(2):
                rope(None, FB[4 + hf][0:n, :], 8, n, i, qro[0:n, hf * 512:(hf + 1) * 512], rqro, rFB[4 + hf],
                     rtmp[0], rtmp[1], rrtmp[0], rrtmp[1])
            for k in range(8):
                P.op("pe", mk("matmul", FB[4][0:n, :], hT[:, k, c0:c0 + n], wq_v[:, k, 1024:1536],
                                                    start=(k == 0), stop=(k == 7)),
                     reads=[rhT[i]] + allWS, writes=[rFB[4]], inc=(k == 7))
            for k in range(8):
                P.op("pe", mk("matmul", FB[5][0:n, 0:8], hT[:, k, c0:c0 + n], wq_v[:, k, 1536:1544],
                                                    start=(k == 0), stop=(k == 7)),
                     reads=[rhT[i]] + allWS, writes=[rFB[5]], inc=(k == 7))
            rope(None, FB[4][0:n, :], 8, n, i, qro[0:n, 1024:1536], rqro, rFB[4], rtmp[0], rtmp[1], rrtmp[0], rrtmp[1])
            P.op("dve", mk("tensor_scalar", out=wi_[bi][0:n, :], in0=FB[5][0:n, 0:8], scalar1=float(NIH ** -0.5),
                                                  scalar2=None, op0=ALU.mult), reads=[rFB[5]], writes=[rwi_[bi]])
            for grp in range(3):
                b = tbi[0] % 2
                tbi[0] += 1
                for c in range(4):
                    cc = grp * 4 + c
                    P.op("pe", mk("transpose",
                        out=TB[b][:, c * 128:c * 128 + n], in_=qro[0:n, cc * 128:(cc + 1) * 128], identity=ident[0:n, 0:n]),
                        reads=[rqro, rconst], writes=[rTB[b]], inc=(c == 3))
                if grp < 2:
                    P.op("act", mk("activation",
                        out=qT_[i % 3][:, grp * 4:(grp + 1) * 4, 0:n],
                        in_=TB[b][:, 0:512].rearrange("p (c t) -> p c t", c=4)[:, :, 0:n], func=AF.Copy),
                        reads=[rTB[b]], writes=[rqT_[i % 3]])
                else:
                    P.op("act", mk("activation",
                        out=qiT[:, :, 0:n], in_=TB[b][:, 0:512].rearrange("p (c t) -> p c t", c=4)[:, :, 0:n], func=AF.Copy),
                        reads=[rTB[b]], writes=[rqiT])
            if cfg.get("stop") == "bq":
                raise StopBuild(nc, es, P)
            nsb = (ladm + 511) // 512
            for sbk in range(nsb):
                k0 = sbk * 512
                kn = min(512, ladm - k0)
                krs = [rK[t] for t in range(k0 // 128, (k0 + kn + 127) // 128)]
                for h in range(NIH):
                    half = h % 2
                    blk = h // 2
                    fb = 4 + (h % 2)
                    P.op("pe", mk("matmul",
                        FB[fb][0:n, 0:kn], qiT[half * 64:(half + 1) * 64, blk, 0:n],
                        kiT2[half * 64:(half + 1) * 64, k0:k0 + kn], start=True, stop=True),
                        reads=[rqiT] + krs, writes=[rFB[fb]], inc=True)
                    rt = h % 2
                    P.op("act", mk("activation",
                        out=rtmp[rt][0:n, 0:kn], in_=FB[fb][0:n, 0:kn], func=AF.Relu, scale=float(IDD ** -0.5)),
                        reads=[rFB[fb]], writes=[rrtmp[rt]])
                    if h == 0:
                        P.op("dve", mk("tensor_scalar",
                            out=score_[bi][0:n, k0:k0 + kn], in0=rtmp[rt][0:n, 0:kn], scalar1=wi_[bi][0:n, 0:1], scalar2=None,
                            op0=ALU.mult), reads=[rrtmp[rt], rwi_[bi]], writes=[rscore_[bi]])
                    else:
                        P.op("dve", mk("scalar_tensor_tensor",
                            out=score_[bi][0:n, k0:k0 + kn], in0=rtmp[rt][0:n, 0:kn], scalar=wi_[bi][0:n, h:h + 1],
                            in1=score_[bi][0:n, k0:k0 + kn], op0=ALU.mult, op1=ALU.add),
                            reads=[rrtmp[rt], rwi_[bi], rscore_[bi]], writes=[rscore_[bi]])
            if cfg.get("stop") == "idx":
                raise StopBuild(nc, es, P)

        def f2(i):
            tl = tiles[i]
            bi = i % 2
            n = tl["n"]
            gi = tl["gi"]
            ladm = gi * 128 + n if kind == "p" else LS
            ksel = KSEL_P if kind == "p" else KSEL_S
            nkb = (ladm + 127) // 128
            need_thr = ladm > ksel
            if need_thr:
                P.op("dve", mk("tensor_reduce", out=bis_[bi][0:n, 0:1], in_=score_[bi][0:n, 0:ladm], axis=AX.X, op=ALU.max),
                     reads=[rscore_[bi]], writes=[rbis_[bi]])
                P.op("dve", mk("tensor_reduce", out=bis_[bi][0:n, 1:2], in_=score_[bi][0:n, 0:ladm], axis=AX.X, op=ALU.min),
                     reads=[rscore_[bi]], writes=[rbis_[bi]])
                P.op("dve", mk("tensor_tensor", out=bis_[bi][0:n, 2:3], in0=bis_[bi][0:n, 0:1], in1=bis_[bi][0:n, 1:2], op=ALU.subtract),
                     reads=[rbis_[bi]], writes=[rbis_[bi]])
                P.op("dve", mk("tensor_scalar", out=bis_[bi][0:n, 2:3], in0=bis_[bi][0:n, 2:3], scalar1=-0.5, scalar2=-1e-5,
                                                      op0=ALU.mult, op1=ALU.add), reads=[rbis_[bi]], writes=[rbis_[bi]])
                P.op("dve", mk("tensor_scalar", out=bisA_[bi][0:n, :], in0=pow2[0:n, :], scalar1=bis_[bi][0:n, 2:3], scalar2=None,
                                                      op0=ALU.mult), reads=[rbis_[bi], rconst], writes=[rbis_[bi]])
                P.op("dve", mk("tensor_scalar", out=bis_[bi][0:n, 3:4], in0=bis_[bi][0:n, 1:2], scalar1=-1.0, scalar2=1e-5,
                                                      op0=ALU.mult, op1=ALU.add), reads=[rbis_[bi]], writes=[rbis_[bi]])
                P.op("dve", mk("tensor_tensor", out=bis_[bi][0:n, 3:4], in0=bis_[bi][0:n, 3:4], in1=bis_[bi][0:n, 2:3], op=ALU.add),
                     reads=[rbis_[bi]], writes=[rbis_[bi]])
            if kind == "p":
                P.op("dve", mk("memset", score_[bi][0:64, ladm - 64:ladm], NEGBIG), reads=[], writes=[rscore_[bi]])
            if need_thr:
                for it in range(NBIS):
                    P.op("act", mk("activation",
                        out=mb_[bi][0:n, 0:ladm], in_=score_[bi][0:n, 0:ladm], func=AF.Sign, bias=bis_[bi][0:n, 3:4], scale=1.0,
                        accum_out=bis_[bi][0:n, 4:5]), reads=[rscore_[bi], rbis_[bi]], writes=[rmb_[bi], rbis_[bi]])
                    P.op("dve", mk("tensor_scalar",
                        out=bis_[bi][0:n, 5:6], in0=bis_[bi][0:n, 4:5], scalar1=float(2 * ksel - ladm), scalar2=bisA_[bi][0:n, it:it + 1],
                        op0=ALU.is_ge, op1=ALU.mult), reads=[rbis_[bi]], writes=[rbis_[bi]])
                    P.op("dve", mk("scalar_tensor_tensor",
                        out=bis_[bi][0:n, 3:4], in0=bis_[bi][0:n, 3:4], scalar=bisA_[bi][0:n, NBIS + 1 + it:NBIS + 2 + it],
                        in1=bis_[bi][0:n, 5:6], op0=ALU.add, op1=ALU.add), reads=[rbis_[bi]], writes=[rbis_[bi]])
                P.op("dve", mk("scalar_tensor_tensor", out=bis_[bi][0:n, 6:7], in0=bis_[bi][0:n, 3:4], scalar=-1.0,
                               in1=bisA_[bi][0:n, NBIS:NBIS + 1], op0=ALU.mult, op1=ALU.add), reads=[rbis_[bi]], writes=[rbis_[bi]])
            else:
                P.op("dve", mk("memset", bis_[bi][0:n, 6:7], -1.0e29), reads=[], writes=[rbis_[bi]])
            P.op("dve", mk("tensor_scalar", out=mb_[bi][0:n, 0:ladm], in0=score_[bi][0:n, 0:ladm], scalar1=bis_[bi][0:n, 6:7],
                                                  scalar2=MASKV, op0=ALU.is_lt, op1=ALU.mult),
                 reads=[rscore_[bi], rbis_[bi]], writes=[rmb_[bi]])
            if cfg.get("stop") == "thr":
                raise StopBuild(nc, es, P)

        def back(i):
            tl = tiles[i]
            c0 = c0s[i]
            bi = i % 2
            n = tl["n"]
            gi = tl["gi"]
            ladm = gi * 128 + n if kind == "p" else LS
            nkb = (ladm + 127) // 128
            lpad = nkb * 128
            if lpad > ladm:
                P.op("dve", mk("memset", mb_[bi][:, ladm:lpad], MASKV), reads=[], writes=[rmb_[bi]])
            steps = [(g, kb) for g in range(NKV) for kb in range(nkb)]
            SB3 = (0, 1, 3)
            LOOK = 2

            def emit_qk(si):
                g, kb = steps[si]
                half = g % 2
                blk0 = 0 if g < 2 else 4
                k0 = kb * 128
                sfb = SB3[si % 3]
                P.op("pe", mk("matmul",
                    FB[sfb][:, :].rearrange("p (c t) -> p c t", c=4),
                    kT2[half * 64:(half + 1) * 64, g // 2, k0:k0 + 128],
                    qT_[i % 3][half * 64:(half + 1) * 64, blk0:blk0 + 4, :], start=True, stop=False),
                    reads=[rqT_[i % 3], rK[kb]], writes=[rFB[sfb]], inc=False)
                P.op("pe", mk("matmul",
                    FB[sfb][:, :].rearrange("p (c t) -> p c t", c=4), mb_[bi][:, k0:k0 + 128],
                    i4[:, :].rearrange("p (c t) -> p c t", c=4), start=False, stop=True),
                    reads=[rmb_[bi], rconst], writes=[rFB[sfb]], inc=True)

            for s0 in range(min(LOOK, len(steps))):
                emit_qk(s0)
            for si, (g, kb) in enumerate(steps):
                sfb = SB3[si % 3]
                pb = si % 3
                ofb = 2
                P.op("act", mk("activation",
                    out=pTb[pb][:, :], in_=FB[sfb][:, :], func=AF.Exp, scale=float(HD ** -0.5)),
                    reads=[rFB[sfb]], writes=[rpTb[pb]])
                if si + LOOK < len(steps):
                    emit_qk(si + LOOK)
                if kb > 0:
                    for _d in range(int(cfg.get("ndummy", 2))):
                        P.op("pe", mk("matmul", FB[ofb][:, 260:512], i4[:, 0:128], i4[:, 0:252], start=False, stop=False),
                             reads=[rconst], writes=[rFB[ofb]], inc=False)
                for jj in range(4):
                    P.op("pe", mk("matmul",
                        FB[ofb][:, jj * 65:(jj + 1) * 65], pTb[pb][:, jj * 128:(jj + 1) * 128],
                        Vp[:, kb, g * 65:(g + 1) * 65], start=(kb == 0 and jj == 0), stop=(kb == nkb - 1 and jj == 3)),
                        reads=[rpTb[pb], rK[kb]], writes=[rFB[ofb]], inc=(jj == 3))
                if kb == nkb - 1:
                    ov = FB[ofb][0:n, 0:260].rearrange("p (j f) -> p j f", j=4)
                    P.op("dve", mk("reciprocal", out=rcp[0:n, :], in_=ov[:, :, 64]), reads=[rFB[ofb]], writes=[rrcp])
                    P.op("dve", mk("tensor_tensor",
                        out=ob[0:n, g * 256:(g + 1) * 256].rearrange("p (j f) -> p j f", j=4), in0=ov[:, :, 0:64],
                        in1=rcp[0:n, :].unsqueeze(2).to_broadcast([n, 4, 64]), op=ALU.mult),
                        reads=[rFB[ofb], rrcp], writes=[rob])
            if cfg.get("stop") == "core":
                raise StopBuild(nc, es, P)
            transpose_to_hT(i, n, ob, rob, c0)

        if kind == "p" and not cfg.get("nopipe"):
            f1(0)
            f2(0)
            if nt > 1:
                f1(1)
            for i in range(nt):
                f2u = P.record(f2, i + 1) if i + 1 < nt else []
                fu = f2u + (P.record(f1, i + 2) if i + 2 < nt else [])
                if fu:
                    bu = P.record(back, i)
                    nf, nb_ = len(fu), len(bu)
                    span = max(1, int(nb_ * float(cfg.get("fspan", 0.9))))
                    fi = 0
                    for j, u in enumerate(bu):
                        tgt = min(nf, ((j + 1) * nf) // span)
                        while fi < tgt:
                            P.replay(fu[fi])
                            fi += 1
                        P.replay(u)
                    while fi < nf:
                        P.replay(fu[fi])
                        fi += 1
                else:
                    back(i)
        else:
            for i in range(nt):
                f1(i)
                f2(i)
                back(i)
        wo_v = WRv(0, 8, 1024)
        load_w(lambda j: wo_v[:, j, :], lambda j: w_o[j * 128:(j + 1) * 128, :], 8, 1024, allWS,
               name="wo", region=WR[:, 0:8192], dsi=4)
        c0 = 0
        for i, tl in enumerate(tiles):
            n = tl["n"]
            for hf in range(2):
                fb = 4 + hf
                for k in range(8):
                    P.op("pe", mk("matmul",
                        FB[fb][0:n, :], hT[:, k, c0:c0 + n], wo_v[:, k, hf * 512:(hf + 1) * 512],
                        start=(k == 0), stop=(k == 7)), reads=[rhT[i]] + allWS, writes=[rFB[fb]], inc=(k == 7))
                P.op("dve", mk("tensor_tensor",
                    out=X[0:n, i, hf * 512:(hf + 1) * 512], in0=FB[fb][0:n, :],
                    in1=X[0:n, i, hf * 512:(hf + 1) * 512], op=ALU.add), reads=[rFB[fb], rX[i]], writes=[rX[i]])
            c0 += n

        if cfg.get("stop") == "attn":
            P.finish()
            return nc, es, P
        norm_to_hT(tiles, 3, 1)
        run_mlp(1, tiles, 0)

        load_g(4, 0)
        norm_stats(tiles, 48)
        for i, tl in enumerate(tiles):
            n = tl["n"]
            s_ = stg_i[0] % 2
            stg_i[0] += 1
            norm_tile(i, n, 48 + i)
            normalize(i, n, 48 + i, 0, stg[s_][0:n, :], rstg[s_])
            P.dma("sp", tl["ydst"], stg[s_][0:n, :], dsStg[s_], reads=[rstg[s_]])

    P.finish()
    return nc, es, P


def make_in_maps(cfg, ncores, x_prompt, x_sample, state_pool, cache_k, cache_v, cache_kidx, norm_mix, norm_mlp,
                 norm_final, pool_w, pool_scale, attn_w_in, attn_w_o, mlp_w_up, mlp_w_down):
    NPS, T, NSS, TS, PL = cfg["NPS"], cfg["T"], cfg["NSS"], cfg["TS"], cfg["PL"]
    f = lambda a: np.ascontiguousarray(np.asarray(a, dtype=np.float32))
    w_in = f(attn_w_in)[0]
    qcols = _q_perm()
    w_kv = np.concatenate([w_in[:, 1024:1280], w_in[:, 2048:2112], w_in[:, 1280:1536]], axis=1)
    w_q = np.concatenate([w_in[:, 0:1024][:, qcols], w_in[:, 1536:2048], w_in[:, 2112:2120]], axis=1)
    g_all = np.stack([f(norm_mix)[0], f(norm_mlp)[0], f(norm_mix)[1], f(norm_mlp)[1], f(norm_final)], axis=0)
    bands = _bands().reshape(20, 128, 128).transpose(1, 0, 2)
    cosp, sinp = _rope_tab(np.arange(T))
    coss, sins = _rope_tab(PL + np.arange(TS))
    pw = np.concatenate([2.0 ** -np.arange(NBIS + 1), -(2.0 ** -(np.arange(NBIS + 1) + 1.0))]).astype(np.float32)
    shared = dict(
        g_all=f(g_all), poolw=f(pool_w)[0], pscale=f(pool_scale)[0], w_kv=f(w_kv), w_q=f(w_q), w_o=f(attn_w_o)[0],
        w_up=f(mlp_w_up), w_dn=f(mlp_w_down),
        c_ident=np.eye(128, dtype=np.float32),
        c_i4=np.ascontiguousarray(np.tile(np.eye(128, dtype=np.float32), (1, 4))),
        c_bands=np.ascontiguousarray(bands).astype(np.float32),
        c_cosp=cosp, c_sinp=sinp, c_coss=coss, c_sins=sins,
        c_pow2=np.ascontiguousarray(np.tile(pw[None, :], (128, 1))),
    )
    xp = f(x_prompt); xs = f(x_sample); sp = f(state_pool)[0]
    ck = f(cache_k)[0].reshape(-1, PL, 256); cv = f(cache_v)[0].reshape(-1, PL, 256); cki = f(cache_kidx)[0]
    maps = []
    for c in range(ncores):
        m = dict(shared)
        m["xp"] = xp[c * NPS:(c + 1) * NPS]
        m["xs"] = xs[c * NSS:(c + 1) * NSS]
        m["spool"] = sp[c * NSS:(c + 1) * NSS]
        m["ck"] = ck[c * NSS:(c + 1) * NSS]
        m["cv"] = cv[c * NSS:(c + 1) * NSS]
        m["cki"] = cki[c * NSS:(c + 1) * NSS]
        maps.append(m)
    return maps


def gather(cfg, results):
    NPS, T, NSS, TS, PL = cfg["NPS"], cfg["T"], cfg["NSS"], cfg["TS"], cfg["PL"]
    cat = lambda k: np.concatenate([np.asarray(r[k], dtype=np.float32) for r in results], axis=0)
    y_p = cat("y_p"); y_s = cat("y_s")
    pool_p = cat("pool_p")[None]; pool_s = cat("pool_s")[None]
    k_p = cat("k_p").reshape(1, -1, T, NKV, HD); v_p = cat("v_p").reshape(1, -1, T, NKV, HD); ki_p = cat("ki_p")[None]
    k_s = cat("k_s").reshape(1, -1, TS, NKV, HD); v_s = cat("v_s").reshape(1, -1, TS, NKV, HD); ki_s = cat("ki_s")[None]
    return (y_p, y_s, pool_p, pool_s, k_p, v_p, ki_p, k_s, v_s, ki_s)


def kernel(**inputs):
    ncores = 8
    B, T, _ = inputs["x_prompt"].shape
    SB, TS, _ = inputs["x_sample"].shape
    PL = inputs["cache_k"].shape[2]
    cfg = dict(NPS=B // ncores, T=T, NSS=SB // ncores, TS=TS, PL=PL)
    nc, es, P = build(cfg)
    maps = make_in_maps(cfg, ncores, **inputs)
    res = run_bass_kernel_spmd(nc, maps, core_ids=list(range(ncores)))
    es.close()
    return gather(cfg, res.results)
```
